# Optimizing a Trainium2 kernel written in Bass

```python
import jax, jax.numpy as jnp
from jax import lax
import numpy as np

D_MODEL = 1024
BATCH = 8
SEQ = 4096
DEPTH = 2

N_HEADS = 16
HEAD_DIM = 64
ATTN_DIM = N_HEADS * HEAD_DIM
ROT_DIM = HEAD_DIM // 4
ROPE_THETA = 500000.0
D_FF = 2816
FFN_RESIDUAL_WEIGHT = 0.5
N_A_LAYERS = DEPTH // 2
N_B_LAYERS = DEPTH - N_A_LAYERS
DILATION_PAIRS = ((128, 1), (512, 4), (2048, 16))
MOBA_BLOCK = 256
MOBA_TOPK = 3
MOBA_Q_CHUNK = 32
RMS_EPS = 1e-6

kernel_name = 'hybrid_dilated_moba_yoco'


def rmsnorm(x, g):
    xf = x.astype(jnp.float32)
    y = xf * lax.rsqrt(jnp.mean(xf * xf, axis=-1, keepdims=True) + RMS_EPS)
    return (y * g.astype(jnp.float32)).astype(x.dtype)


def swiglu(h, w_gate, w_up, w_down):
    return (jax.nn.silu(h @ w_gate) * (h @ w_up)) @ w_down


def partial_rotary(t, pos):
    half = ROT_DIM // 2
    inv_freq = ROPE_THETA ** (-jnp.arange(half, dtype=jnp.float32) / half)
    ang = pos.astype(jnp.float32)[:, None] * inv_freq[None, :]
    cos, sin = jnp.cos(ang), jnp.sin(ang)
    tf = t.astype(jnp.float32)
    x1, x2 = tf[..., :half], tf[..., half:ROT_DIM]
    out = jnp.concatenate([x1 * cos - x2 * sin, x1 * sin + x2 * cos, tf[..., ROT_DIM:]], axis=-1)
    return out.astype(t.dtype)


def dilated_branch(q, k, v, window, dil):
    B, H, S, Dh = q.shape
    blk = window // dil
    span = blk * dil
    s_pad = -(-S // span) * span
    L = s_pad // dil
    nb = L // blk

    def to_sub(t):
        t = jnp.pad(t, ((0, 0), (0, 0), (0, s_pad - S), (0, 0)))
        t = t.reshape(B, H, L, dil, Dh).transpose(0, 1, 3, 2, 4)
        return t.reshape(B, H, dil, nb, blk, Dh)

    qs, ks, vs = to_sub(q), to_sub(k), to_sub(v)

    def with_prev(t):
        prev = jnp.pad(t, ((0, 0), (0, 0), (0, 0), (1, 0), (0, 0), (0, 0)))[:, :, :, :-1]
        return jnp.concatenate([prev, t], axis=4)

    kk, vv = with_prev(ks), with_prev(vs)
    s = jnp.einsum('bhrnqd,bhrnkd->bhrnqk', qs, kk).astype(jnp.float32) * (HEAD_DIM ** -0.5)
    qi = jnp.arange(blk)[:, None]
    kj = jnp.arange(2 * blk)[None, :]
    diff = blk + qi - kj
    band = (diff >= 0) & (diff <= blk)
    has_prev = (jnp.arange(nb)[:, None, None] > 0) | (kj[None] >= blk)
    mask = band[None] & has_prev
    s = jnp.where(mask, s, -jnp.inf)
    m = jnp.max(s, axis=-1, keepdims=True)
    p = jnp.exp(s - m)
    l = jnp.sum(p, axis=-1, keepdims=True)
    o = jnp.einsum('bhrnqk,bhrnkd->bhrnqd', (p / l).astype(v.dtype), vv)
    lse = m + jnp.log(l)

    def from_sub(t):
        X = t.shape[-1]
        t = t.reshape(B, H, dil, L, X).transpose(0, 1, 3, 2, 4).reshape(B, H, s_pad, X)
        return t[:, :, :S]

    return from_sub(o), from_sub(lse)[..., 0]


def mixer_a(h, w_qkv, w_o, pos):
    B, S, _ = h.shape
    qkv = (h @ w_qkv).reshape(B, S, 3, N_HEADS, HEAD_DIM).transpose(2, 0, 3, 1, 4)
    q, k, v = partial_rotary(qkv[0], pos), partial_rotary(qkv[1], pos), qkv[2]
    outs, lses = [], []
    for window, dil in DILATION_PAIRS:
        o_i, lse_i = dilated_branch(q, k, v, window, dil)
        outs.append(o_i)
        lses.append(lse_i)
    wts = jax.nn.softmax(jnp.stack(lses, axis=0), axis=0)
    o = jnp.sum(wts[..., None] * jnp.stack(outs, axis=0).astype(jnp.float32), axis=0).astype(h.dtype)
    return o.transpose(0, 2, 1, 3).reshape(B, S, ATTN_DIM) @ w_o


def shared_kv(h_stream, kv_norm, kv_w, pos):
    B, S, _ = h_stream.shape
    hn = rmsnorm(h_stream, kv_norm)
    kv = (hn @ kv_w).reshape(B, S, 2, N_HEADS, HEAD_DIM).transpose(2, 0, 3, 1, 4)
    k, v = partial_rotary(kv[0], pos), kv[1]
    s_pad = -(-S // MOBA_BLOCK) * MOBA_BLOCK
    pad = ((0, 0), (0, 0), (0, s_pad - S), (0, 0))
    k_pad, v_pad = jnp.pad(k, pad), jnp.pad(v, pad)
    nb = s_pad // MOBA_BLOCK
    k_mean = jnp.mean(k_pad.reshape(B, N_HEADS, nb, MOBA_BLOCK, HEAD_DIM).astype(jnp.float32), axis=3)
    return k_pad, v_pad, k_mean.astype(k.dtype)


def mixer_b(h, w_q, w_o, k_pad, v_pad, k_mean, pos):
    B, S, _ = h.shape
    q = (h @ w_q).reshape(B, S, N_HEADS, HEAD_DIM).transpose(0, 2, 1, 3)
    q = partial_rotary(q, pos)
    nb = k_mean.shape[2]
    ksel = min(MOBA_TOPK, nb)
    k_blocks = k_pad.reshape(B, N_HEADS, nb, MOBA_BLOCK, HEAD_DIM)
    v_blocks = v_pad.reshape(B, N_HEADS, nb, MOBA_BLOCK, HEAD_DIM)
    n_chunks = S // MOBA_Q_CHUNK
    q_chunks = q.reshape(B, N_HEADS, n_chunks, MOBA_Q_CHUNK, HEAD_DIM).transpose(2, 0, 1, 3, 4)
    gather = jax.vmap(jax.vmap(lambda blocks, idx: blocks[idx]))
    scale = HEAD_DIM ** -0.5

    def chunk_attend(args):
        c, qc = args
        t = c * MOBA_Q_CHUNK + jnp.arange(MOBA_Q_CHUNK)
        own = (c * MOBA_Q_CHUNK) // MOBA_BLOCK
        gate = jnp.einsum('bhqd,bhnd->bhqn', qc, k_mean).astype(jnp.float32)
        gate = jnp.where(jnp.arange(nb) < own, gate, -jnp.inf)
        _, sel = lax.top_k(gate, ksel)
        valid = sel < own
        k_s = gather(k_blocks, sel)
        v_s = gather(v_blocks, sel)
        s_sel = jnp.einsum('bhqd,bhqnkd->bhqnk', qc, k_s).astype(jnp.float32) * scale
        s_sel = jnp.where(valid[..., None], s_sel, -jnp.inf).reshape(B, N_HEADS, MOBA_Q_CHUNK, ksel * MOBA_BLOCK)
        k_o = lax.dynamic_slice_in_dim(k_pad, own * MOBA_BLOCK, MOBA_BLOCK, axis=2)
        v_o = lax.dynamic_slice_in_dim(v_pad, own * MOBA_BLOCK, MOBA_BLOCK, axis=2)
        s_own = jnp.einsum('bhqd,bhkd->bhqk', qc, k_o).astype(jnp.float32) * scale
        causal = (own * MOBA_BLOCK + jnp.arange(MOBA_BLOCK))[None, :] <= t[:, None]
        s_own = jnp.where(causal, s_own, -jnp.inf)
        p = jax.nn.softmax(jnp.concatenate([s_sel, s_own], axis=-1), axis=-1).astype(qc.dtype)
        p_sel = p[..., :ksel * MOBA_BLOCK].reshape(B, N_HEADS, MOBA_Q_CHUNK, ksel, MOBA_BLOCK)
        p_own = p[..., ksel * MOBA_BLOCK:]
        return (jnp.einsum('bhqnk,bhqnkd->bhqd', p_sel, v_s)
                + jnp.einsum('bhqk,bhkd->bhqd', p_own, v_o))

    out = lax.map(chunk_attend, (jnp.arange(n_chunks), q_chunks))
    out = out.transpose(1, 0, 3, 2, 4).reshape(B, S, ATTN_DIM)
    return out @ w_o


def setup_inputs(seed: int = 0) -> dict:
    key = jax.random.key(seed)
    ks = jax.random.split(key, 18)
    f32 = jnp.float32

    def w(k, shape, fan_in):
        return jax.random.normal(k, shape, f32) * (fan_in ** -0.5)

    def gain(k, shape):
        return 1.0 + 0.02 * jax.random.normal(k, shape, f32)

    return {
        'x': jax.random.normal(ks[0], (BATCH, SEQ, D_MODEL), f32),
        'ffn1_norm': gain(ks[1], (DEPTH, D_MODEL)),
        'ffn1_w_gate': w(ks[2], (DEPTH, D_MODEL, D_FF), D_MODEL),
        'ffn1_w_up': w(ks[3], (DEPTH, D_MODEL, D_FF), D_MODEL),
        'ffn1_w_down': w(ks[4], (DEPTH, D_FF, D_MODEL), D_FF),
        'mix_norm': gain(ks[5], (DEPTH, D_MODEL)),
        'ffn2_norm': gain(ks[6], (DEPTH, D_MODEL)),
        'ffn2_w_gate': w(ks[7], (DEPTH, D_MODEL, D_FF), D_MODEL),
        'ffn2_w_up': w(ks[8], (DEPTH, D_MODEL, D_FF), D_MODEL),
        'ffn2_w_down': w(ks[9], (DEPTH, D_FF, D_MODEL), D_FF),
        'a_w_qkv': w(ks[10], (N_A_LAYERS, D_MODEL, 3 * ATTN_DIM), D_MODEL),
        'a_w_o': w(ks[11], (N_A_LAYERS, ATTN_DIM, D_MODEL), ATTN_DIM),
        'kv_norm': gain(ks[12], (D_MODEL,)),
        'kv_w': w(ks[13], (D_MODEL, 2 * ATTN_DIM), D_MODEL),
        'b_w_q': w(ks[14], (N_B_LAYERS, D_MODEL, ATTN_DIM), D_MODEL),
        'b_w_o': w(ks[15], (N_B_LAYERS, ATTN_DIM, D_MODEL), ATTN_DIM),
        'final_norm': gain(ks[16], (D_MODEL,)),
    }


def reference(x, ffn1_norm, ffn1_w_gate, ffn1_w_up, ffn1_w_down, mix_norm,
              ffn2_norm, ffn2_w_gate, ffn2_w_up, ffn2_w_down, a_w_qkv, a_w_o,
              kv_norm, kv_w, b_w_q, b_w_o, final_norm):
    pos = jnp.arange(x.shape[1], dtype=jnp.int32)
    h = x
    shared = None
    for layer in range(DEPTH):
        h = h + FFN_RESIDUAL_WEIGHT * swiglu(rmsnorm(h, ffn1_norm[layer]),
                                             ffn1_w_gate[layer], ffn1_w_up[layer], ffn1_w_down[layer])
        hn = rmsnorm(h, mix_norm[layer])
        if layer < N_A_LAYERS:
            h = h + mixer_a(hn, a_w_qkv[layer], a_w_o[layer], pos)
        else:
            j = layer - N_A_LAYERS
            h = h + mixer_b(hn, b_w_q[j], b_w_o[j], shared[0], shared[1], shared[2], pos)
        h = h + FFN_RESIDUAL_WEIGHT * swiglu(rmsnorm(h, ffn2_norm[layer]),
                                             ffn2_w_gate[layer], ffn2_w_up[layer], ffn2_w_down[layer])
        if layer == N_A_LAYERS - 1:
            shared = shared_kv(h, kv_norm, kv_w, pos)
    return rmsnorm(h, final_norm)
```

```python
import numpy as np
import concourse.bass as bass
import concourse.mybir as mybir
from concourse.bass_utils import run_bass_kernel_spmd

F32 = mybir.dt.float32
BF16 = mybir.dt.bfloat16
AF = mybir.ActivationFunctionType
ALU = mybir.AluOpType
AX = mybir.AxisListType

S = 4096
D = 1024
H = 16
FF = 2816
NJ = 22
T = 512
NT = S // T
EPS = 1e-6
NEG = -30000.0
ENG = ('pe', 'act', 'dve', 'pool', 'sp')


class Res:
    __slots__ = ('name', 'w', 'rc', 'rd')

    def __init__(self, name):
        self.name = name
        self.w = None
        self.rc = {}
        self.rd = []


class DS:
    def __init__(self, nc, name):
        self.h = nc.semaphore(name).__enter__()
        self.n = 0


class Op:
    __slots__ = ('eng', 'fn', 'deps', 'dsem', 'val', 'inc', 'ndma', 'amt')


class Sched:
    def __init__(self, nc):
        self.nc = nc
        self.ops = {e: [] for e in ENG}
        self.pending_dma = []
        self.nds = 0

    def dsem(self, name):
        self.nds += 1
        return DS(self.nc, "%s_%d" % (name, self.nds))

    def op(self, eng, fn, reads=(), writes=(), dsem=None, ndma=1, extra=(), amt=16):
        o = Op()
        o.eng = eng
        o.fn = fn
        o.dsem = dsem
        o.inc = False
        o.val = None
        o.ndma = ndma
        o.amt = amt
        deps = list(extra)
        for r in reads:
            if r.w is not None:
                deps.append(r.w)
        for w in writes:
            if w.w is not None:
                deps.append(w.w)
            deps.extend(w.rc.values())
            deps.extend(w.rd)
        best = {}
        for d in deps:
            if d is o:
                continue
            if d.dsem is None:
                if d.eng == 'pe' and eng == 'pe' and dsem is None:
                    continue
                key = ('e', d.eng)
                idx = d.val
            else:
                key = ('d', id(d.dsem))
                idx = d.val
            if key not in best or best[key].val < idx:
                best[key] = d
        o.deps = list(best.values())
        for d in o.deps:
            d.inc = True
        if dsem is not None:
            dsem.n += amt * ndma
            o.val = dsem.n
            self.pending_dma.append(o)
        else:
            o.val = len(self.ops[eng])
        for r in reads:
            if dsem is None:
                r.rc[eng] = o
            else:
                r.rd.append(o)
        for w in writes:
            w.w = o
            w.rc = {}
            w.rd = []
        self.ops[eng].append(o)
        return o

    def barrier(self):
        tails = []
        for e in ENG:
            for o in reversed(self.ops[e]):
                if o.dsem is None:
                    tails.append(o)
                    break
        pend = self.pending_dma
        self.pending_dma = []
        for e in ENG:
            self.op(e, lambda eng: None, extra=tails + pend)

    def emit(self):
        nc = self.nc
        esem = {e: nc.semaphore('sem_' + e).__enter__() for e in ENG}
        for e in ENG:
            k = 0
            for o in self.ops[e]:
                if o.dsem is None:
                    if o.inc:
                        k += 1
                        o.val = k
                    else:
                        o.val = None
        ops = self.ops

        def run(e, eng):
            seen = {}
            for o in ops[e]:
                for d in o.deps:
                    if d.dsem is not None:
                        sem = d.dsem.h
                        key = id(d.dsem)
                    else:
                        sem = esem[d.eng]
                        key = d.eng
                    if seen.get(key, 0) < d.val:
                        eng.wait_ge(sem, d.val)
                        seen[key] = d.val
                ins = o.fn(eng)
                if o.dsem is not None:
                    assert len(ins) == o.ndma, (len(ins), o.ndma)
                    for i in ins:
                        if o.amt == 16:
                            i.then_inc(o.dsem.h, 16)
                        else:
                            i.then_inc(o.dsem.h)
                elif o.inc:
                    if ins is None:
                        ins = eng.nop()
                    ins.then_inc(esem[e], 1)

        with nc.Block() as block:
            @block.tensor
            def _(t):
                run('pe', t)

            @block.scalar
            def _(a):
                run('act', a)

            @block.vector
            def _(v):
                run('dve', v)

            @block.gpsimd
            def _(g):
                run('pool', g)

            @block.sync
            def _(s):
                run('sp', s)


WSPEC = {}
for _l in (0, 1):
    for _a in (1, 2):
        WSPEC['g%d%d' % (_l, _a)] = (D, FF)
        WSPEC['u%d%d' % (_l, _a)] = (D, FF)
        WSPEC['d%d%d' % (_l, _a)] = (FF, D)
WSPEC['wqkv'] = (D, 3584)
WSPEC['woa'] = (D, D)
WSPEC['wkv'] = (D, 2304)
WSPEC['wqb'] = (D, 1280)
WSPEC['wob'] = (D, D)
WORDER = ['g01', 'u01', 'd01', 'wqkv', 'woa', 'g02', 'u02', 'd02', 'wkv', 'g11', 'u11', 'd11', 'wqb',
          'wob', 'g12', 'u12', 'd12']
FFN_PIECES = [(0, 512), (512, 512), (1024, 512), (1536, 512), (2048, 512), (2560, 256)]


def build(stop_after=None, debug=False):
    nc = bass.Bass("TRN2", target_bir_lowering=False)
    sch = Sched(nc)
    okind = "ExternalOutput" if debug else "Internal"

    def din(name, shape, dt=F32):
        return nc.dram_tensor(name, shape, dt, kind="ExternalInput").ap()

    xT = din('xT', [D, S])
    wext = {n: din(n, [WSPEC[n][0] // 8, WSPEC[n][1]]) for n in WORDER}
    wslb = {n: nc.dram_tensor('wl_' + n, [WSPEC[n][0] // 8, WSPEC[n][1]], BF16) for n in WORDER}
    gains_d = din('gains', [128, 64])
    rotC_e = din('rotC', [16, S])
    rotS_e = din('rotS', [16, S])
    rotC_b = nc.dram_tensor('rotC_b', [16, S], F32)
    rotS_b = nc.dram_tensor('rotS_b', [16, S], F32)
    rotC_t = nc.dram_tensor('rotC_f', [128, S], F32)
    rotS_t = nc.dram_tensor('rotS_f', [128, S], F32)
    rotC_d = rotC_t.ap()
    rotS_d = rotS_t.ap()
    maskA_d = din('maskA', [128, 256])
    maskB_d = din('maskB', [128, 512])
    ident_d = din('ident', [128, 128])
    blk1h_d = din('blk1h', [16, S])
    gbias_d = din('gbias', [128, 512])
    yT = nc.dram_tensor('yT', [D, S], F32, kind="ExternalOutput").ap()

    wscr_t = {n: nc.dram_tensor('ws_' + n, list(WSPEC[n]), BF16) for n in WORDER}
    wscr = {n: wscr_t[n].ap() for n in WORDER}
    hscr = nc.dram_tensor('hscr', [NT, 128, 8, T], F32, kind=okind).ap()
    qT_s = nc.dram_tensor('qT_s', [D, S], BF16, kind=okind).ap()
    kT_s = nc.dram_tensor('kT_s', [D, S], BF16, kind=okind).ap()
    v_s = nc.dram_tensor('v_s', [S, D], BF16, kind=okind).ap()
    kT2_s = nc.dram_tensor('kT2_s', [D, S], BF16, kind=okind).ap()
    v2_s = nc.dram_tensor('v2_s', [S, D], BF16, kind=okind).ap()
    at_s = nc.dram_tensor('at_s', [D, S], BF16, kind=okind).ap()
    km_s = nc.dram_tensor('km_s', [D, 16], BF16, kind=okind).ap()

    def sb(name, shape, dt):
        return nc.alloc_sbuf_tensor(name, shape, dt)

    ident = sb('ident_sb', [128, 128], BF16)
    ones_bf = sb('ones_bf', [128, 128], BF16)
    ones_f = sb('ones_f', [128, 64], F32)
    maskA = sb('maskA_sb', [128, 256], BF16)
    maskB = sb('maskB_sb', [128, 512], BF16)
    gains = sb('gains_sbuf', [128, 64], F32)
    gbias = sb('gbias_sbuf', [128, 512], F32)
    epsc = sb('epsc', [128, 1], F32)
    kms = sb('kms', [128, 8, 16], F32)
    NBF = 70656
    NFA = 11776
    BFA = sb('BFA', [128, NBF], BF16)
    FA = sb('FA', [128, NFA], F32)
    PS = [nc.alloc_psum_tensor('ps%d' % i, [128, 512], F32) for i in range(8)]
    PSR = [Res('ps%d' % i) for i in range(8)]

    def bview(off, n):
        return BFA[:, off:off + n]

    def fview(off, n):
        return FA[:, off:off + n]

    def v3(ap, a):
        return ap.rearrange("p (a b) -> p a b", a=a)

    const_res = Res('const')
    cds = sch.dsem('const')

    def const_fn(g):
        ins = []
        ins.append(g.dma_start(out=ident[:], in_=ident_d[:, :]))
        ins.append(g.dma_start(out=maskA[:], in_=maskA_d[:, :]))
        ins.append(g.dma_start(out=maskB[:], in_=maskB_d[:, :]))
        ins.append(g.dma_start(out=gains[:], in_=gains_d[:, :]))
        ins.append(g.dma_start(out=gbias[:], in_=gbias_d[:, :]))
        return ins
    sch.op('pool', const_fn, writes=[const_res], dsem=cds, ndma=5)

    def memset_fn(g):
        g.memset(ones_bf[:], 1.0)
        g.memset(ones_f[:], 1.0)
        g.memset(kms[:], 0.0)
        return g.memset(epsc[:], EPS)
    c2 = Res('const2')
    sch.op('pool', memset_fn, writes=[c2])
    CONST = [const_res, c2]

    wres = {}
    r_chain = Res('agchain')
    RG = [list(range(8))]

    def cast_op(n):
        r = Res('wl_' + n)
        ds = sch.dsem('wc_' + n)
        sch.op('pool', lambda g, n=n: [g.dma_start(out=wslb[n].ap()[:, :], in_=wext[n][:, :])], writes=[r], dsem=ds)
        return r

    def ag_op(src_t, dst_t, r_src, name):
        r = Res('w_' + name)
        ds = sch.dsem('ag_' + name)
        sch.op('pool', lambda g: [g.collective_compute("AllGather", ALU.bypass, replica_groups=RG,
                                                       ins=[src_t.ap().opt()], outs=[dst_t.ap().opt()])],
               reads=[r_src], writes=[r, r_chain], dsem=ds, amt=1)
        return r

    first = WORDER[:3]
    rl = {n: cast_op(n) for n in first}
    for n in first:
        wres[n] = ag_op(wslb[n], wscr_t[n], rl[n], n)
    r_rb = Res('rotb')
    sch.op('pool', lambda g: [g.dma_start(out=rotC_b.ap()[:, :], in_=rotC_e[:, :]),
                              g.dma_start(out=rotS_b.ap()[:, :], in_=rotS_e[:, :])], writes=[r_rb],
           dsem=sch.dsem('rotb'), ndma=2)
    r_rotC = ag_op(rotC_b, rotC_t, r_rb, 'rotC')
    r_rotS = ag_op(rotS_b, rotS_t, r_rb, 'rotS')
    rest = WORDER[3:]
    rl = {n: cast_op(n) for n in rest}
    for n in rest:
        wres[n] = ag_op(wslb[n], wscr_t[n], rl[n], n)

    class SP:
        pass
    sp = SP()
    o = 0
    sp.hnT = v3(bview(o, 4096), 8); o += 4096
    sp.actT = v3(bview(o, 11264), 22); o += 11264
    sp.ring = [v3(bview(o + k * 4096, 4096), 8) for k in range(4)]; o += 16384
    sp.wd = v3(bview(o, 22528), 22); o += 22528
    sp.stage = [bview(o + k * 4096, 4096) for k in range(4)]; o += 16384
    assert o <= NBF
    f = 0
    sp.hT = [v3(fview(f + k * 4096, 4096), 8) for k in range(2)]; f += 8192
    sp.C = fview(f, 512); f += 512
    sp.Sg = fview(f, 512); f += 512
    sp.rs = fview(f, 512); f += 512
    sp.sg = [fview(f + k * 512, 512) for k in range(2)]; f += 1024
    sp.t1 = fview(f, 512); f += 512
    sp.t2 = fview(f, 512); f += 512
    assert f <= NFA

    def new_sp_res():
        sp.r_hn = Res('hn')
        sp.r_act = Res('act')
        sp.r_ring = [Res('ring%d' % k) for k in range(4)]
        sp.r_wd = Res('wd')
        sp.r_stage = [Res('stage%d' % k) for k in range(4)]
        sp.r_hT = [Res('hT%d' % k) for k in range(2)]
        sp.r_tab = Res('tab')
        sp.r_rs = Res('rs')
        sp.r_sg = [Res('sg%d' % k) for k in range(2)]
        sp.r_t1 = Res('t1')
        sp.r_t2 = Res('t2')
        sp.ds_ring = [sch.dsem('ring') for k in range(4)]
        sp.ds_wd = sch.dsem('wd')
        sp.ds_hl = [sch.dsem('hl') for k in range(2)]
        sp.ds_hs = [sch.dsem('hs') for k in range(2)]
        sp.ds_tab = sch.dsem('tab')
        sp.ds_st = [sch.dsem('st') for k in range(4)]
        sp.ring_i = 0
        sp.psrot = 0
        sp.steps = []

    def norm(hT, r_h, nidx, final=False):
        sq = sp.actT

        def f_sq(a):
            ins = None
            for c in range(8):
                ins = a.activation(out=sq[:, c, :], in_=hT[:, c, :], func=AF.Square)
            return ins
        sch.op('act', f_sq, reads=[r_h], writes=[sp.r_act])

        def f_mm(t):
            ins = None
            for c in range(8):
                ins = t.matmul(PS[6][:], ones_bf[:], sq[:, c, :], start=(c == 0), stop=(c == 7))
            return ins
        sch.op('pe', f_mm, reads=[sp.r_act] + CONST, writes=[PSR[6]])
        sch.op('act', lambda a: a.activation(out=sp.rs, in_=PS[6][:], func=AF.Sqrt, bias=epsc[:, 0:1], scale=1.0 / D),
               reads=[PSR[6]] + CONST, writes=[sp.r_rs])
        sch.op('dve', lambda v: v.reciprocal(out=sp.rs, in_=sp.rs), reads=[sp.r_rs], writes=[sp.r_rs])

        def f_n(v):
            ins = None
            for c in range(8):
                outp = hT[:, c, :] if final else sp.hnT[:, c, :]
                ins = v.scalar_tensor_tensor(out=outp, in0=hT[:, c, :], scalar=gains[:, nidx * 8 + c:nidx * 8 + c + 1],
                                             in1=sp.rs, op0=ALU.mult, op1=ALU.mult)
            return ins
        if final:
            sch.op('dve', f_n, reads=[r_h, sp.r_rs] + CONST, writes=[r_h])
        else:
            sch.op('dve', f_n, reads=[r_h, sp.r_rs] + CONST, writes=[sp.r_hn])

    def load_piece(wname, col0, w):
        k = sp.ring_i % 4
        sp.ring_i += 1
        src = wscr[wname].rearrange("(c p) n -> p c n", p=128)[:, :, col0:col0 + w]
        dst = sp.ring[k][:, :, 0:w]
        sch.op('sp', lambda s: [s.dma_start(out=dst, in_=src)], reads=[wres[wname]], writes=[sp.r_ring[k]],
               dsem=sp.ds_ring[k])
        return k

    def next_ps():
        b = (4, 5, 7)[sp.psrot % 3]
        sp.psrot += 1
        return b

    def run_steps():
        steps = sp.steps
        sp.steps = []
        ctxs = [None] * len(steps)
        if steps:
            ctxs[0] = steps[0][0]()
        for i in range(len(steps)):
            if i + 1 < len(steps):
                ctxs[i + 1] = steps[i + 1][0]()
            steps[i][1](ctxs[i])

    def ffn_steps(l, a, hT, r_h):
        gname, uname, dname = 'g%d%d' % (l, a), 'u%d%d' % (l, a), 'd%d%d' % (l, a)
        for pi, (col0, w) in enumerate(FFN_PIECES):
            def ld(pi=pi, col0=col0, w=w):
                kg = load_piece(gname, col0, w)
                ku = load_piece(uname, col0, w)
                if pi == 0:
                    wsrc = wscr[dname].rearrange("(j p) n -> p j n", p=128)
                    parts = [(0, 6), (6, 12), (12, 17), (17, 22)]
                    sch.op('sp', lambda s: [s.dma_start(out=sp.wd[:, a0:a1, :], in_=wsrc[:, a0:a1, :]) for a0, a1 in parts],
                           reads=[wres[dname]], writes=[sp.r_wd], dsem=sp.ds_wd, ndma=4)
                return (kg, ku)

            def cp(ctx, col0=col0, w=w):
                kg, ku = ctx
                for jj in range(w // 128):
                    j = col0 // 128 + jj
                    pg, pu = j % 2, 2 + j % 2

                    def f_g(t, kg=kg, jj=jj, pg=pg):
                        ins = None
                        for c in range(8):
                            ins = t.matmul(PS[pg][:], sp.ring[kg][:, c, jj * 128:(jj + 1) * 128], sp.hnT[:, c, :],
                                           start=(c == 0), stop=(c == 7))
                        return ins
                    sch.op('pe', f_g, reads=[sp.r_ring[kg], sp.r_hn], writes=[PSR[pg]])

                    def f_u(t, ku=ku, jj=jj, pu=pu):
                        ins = None
                        for c in range(8):
                            ins = t.matmul(PS[pu][:], sp.ring[ku][:, c, jj * 128:(jj + 1) * 128], sp.hnT[:, c, :],
                                           start=(c == 0), stop=(c == 7))
                        return ins
                    sch.op('pe', f_u, reads=[sp.r_ring[ku], sp.r_hn], writes=[PSR[pu]])
                    sgi = j % 2
                    sch.op('act', lambda a_, pg=pg, sgi=sgi: a_.activation(out=sp.sg[sgi], in_=PS[pg][:], func=AF.Silu),
                           reads=[PSR[pg]], writes=[sp.r_sg[sgi]])
                    sch.op('dve', lambda v, j=j, pu=pu, sgi=sgi: v.tensor_tensor(out=sp.actT[:, j, :], in0=sp.sg[sgi],
                                                                                  in1=PS[pu][:], op=ALU.mult),
                           reads=[sp.r_sg[sgi], PSR[pu]], writes=[sp.r_act])
            sp.steps.append((ld, cp))

        def cp_down(ctx):
            for m in range(8):
                b = next_ps()

                def f_d(t, m=m, b=b):
                    ins = None
                    for j in range(NJ):
                        ins = t.matmul(PS[b][:], sp.wd[:, j, m * 128:(m + 1) * 128], sp.actT[:, j, :],
                                       start=(j == 0), stop=(j == NJ - 1))
                    return ins
                sch.op('pe', f_d, reads=[sp.r_wd, sp.r_act], writes=[PSR[b]])
                sch.op('dve', lambda v, m=m, b=b: v.scalar_tensor_tensor(out=hT[:, m, :], in0=PS[b][:], scalar=0.5,
                                                                       in1=hT[:, m, :], op0=ALU.mult, op1=ALU.add),
                       reads=[PSR[b], r_h], writes=[r_h])
        sp.steps.append((lambda: None, cp_down))

    def proj_fm_steps(src, r_src, wname, pieces, evac):
        for (col0, w, ch0) in pieces:
            def ld(col0=col0, w=w):
                return load_piece(wname, col0, w)

            def cp(k, w=w, ch0=ch0):
                for mi in range(w // 128):
                    b = next_ps()

                    def f_p(t, k=k, mi=mi, b=b):
                        ins = None
                        for c in range(8):
                            ins = t.matmul(PS[b][:], sp.ring[k][:, c, mi * 128:(mi + 1) * 128], src[:, c, :],
                                           start=(c == 0), stop=(c == 7))
                        return ins
                    sch.op('pe', f_p, reads=[sp.r_ring[k], r_src], writes=[PSR[b]])
                    evac(ch0 + mi, b)
            sp.steps.append((ld, cp))

    def rot_steps(src, r_src, wname, col0, stage_v, r_stage, kmean_tile=None):
        def ld():
            return load_piece(wname, col0, 512)

        def cp(k):
            for mi in range(2):
                ba, bb = next_ps(), 6

                def f_a(t, k=k, mi=mi, ba=ba):
                    ins = None
                    for c in range(8):
                        ins = t.matmul(PS[ba][:], sp.ring[k][:, c, mi * 128:(mi + 1) * 128], src[:, c, :],
                                       start=(c == 0), stop=(c == 7))
                    return ins
                sch.op('pe', f_a, reads=[sp.r_ring[k], r_src], writes=[PSR[ba]])

                def f_b(t, k=k, mi=mi):
                    ins = None
                    for c in range(8):
                        ins = t.matmul(PS[6][:], sp.ring[k][:, c, (mi + 2) * 128:(mi + 3) * 128], src[:, c, :],
                                       start=(c == 0), stop=(c == 7))
                    return ins
                sch.op('pe', f_b, reads=[sp.r_ring[k], r_src], writes=[PSR[6]])
                sch.op('dve', lambda v, ba=ba: v.tensor_tensor(out=sp.t1, in0=PS[ba][:], in1=sp.C, op=ALU.mult),
                       reads=[PSR[ba], sp.r_tab], writes=[sp.r_t1])
                sch.op('dve', lambda v: v.tensor_tensor(out=sp.t2, in0=PS[6][:], in1=sp.Sg, op=ALU.mult),
                       reads=[PSR[6], sp.r_tab], writes=[sp.r_t2])
                sch.op('dve', lambda v: v.tensor_tensor(out=sp.t1, in0=sp.t1, in1=sp.t2, op=ALU.add),
                       reads=[sp.r_t1, sp.r_t2], writes=[sp.r_t1])
                sch.op('act', lambda a_, mi=mi: a_.copy(out=stage_v[:, mi, :], in_=sp.t1),
                       reads=[sp.r_t1], writes=[r_stage])
                if kmean_tile is not None:
                    i = kmean_tile
                    sch.op('dve', lambda v, mi=mi, i=i: v.tensor_reduce(out=kms[:, mi, 2 * i:2 * i + 2],
                                                                        in_=sp.t1.rearrange("p (b t) -> p b t", b=2),
                                                                        axis=AX.X, op=ALU.add),
                           reads=[sp.r_t1], writes=[r_kms])
        sp.steps.append((ld, cp))

    def v_steps(src, r_src, wname, col0, stage_v, r_stage):
        for hf in range(2):
            def ld(hf=hf):
                return load_piece(wname, col0 + hf * 512, 512)

            def cp(k, hf=hf):
                for s_ in range(4):
                    b = next_ps()

                    def f_v(t, k=k, s_=s_, b=b):
                        ins = None
                        for c in range(8):
                            ins = t.matmul(PS[b][:], src[:, c, s_ * 128:(s_ + 1) * 128], sp.ring[k][:, c, :],
                                           start=(c == 0), stop=(c == 7))
                        return ins
                    sch.op('pe', f_v, reads=[sp.r_ring[k], r_src], writes=[PSR[b]])
                    sch.op('act', lambda a_, s_=s_, hf=hf, b=b: a_.copy(out=stage_v[:, s_, hf * 512:(hf + 1) * 512],
                                                                     in_=PS[b][:]),
                           reads=[PSR[b]], writes=[r_stage])
            sp.steps.append((ld, cp))

    r_ser = Res('ser')

    def evac_copy(stage_v, r_stage, kmean_tile=None):
        def ev(ch, b):
            sch.op('act', lambda a_, ch=ch, b=b: a_.copy(out=stage_v[:, ch, :], in_=PS[b][:]),
                   reads=[PSR[b]], writes=[r_stage] + ([r_ser] if kmean_tile is not None else []))
            if kmean_tile is not None:
                i = kmean_tile
                sch.op('dve', lambda v, ch=ch, b=b, i=i: v.tensor_reduce(out=kms[:, ch, 2 * i:2 * i + 2],
                                                                        in_=PS[b][:].rearrange("p (b t) -> p b t", b=2),
                                                                        axis=AX.X, op=ALU.add),
                       reads=[PSR[b], r_ser], writes=[r_kms])
        return ev

    def store_fm(stage_v, r_stage, ds, dst, i):
        d = dst.rearrange("(c p) t -> p c t", p=128)[:, :, i * T:(i + 1) * T]
        sch.op('sp', lambda s: [s.dma_start(out=d, in_=stage_v)], reads=[r_stage], writes=[], dsem=ds)

    def store_v(stage_v, r_stage, ds, dst, i):
        d = dst[i * T:(i + 1) * T, :].rearrange("(s p) n -> p s n", p=128)
        sch.op('sp', lambda s: [s.dma_start(out=d, in_=stage_v)], reads=[r_stage], writes=[], dsem=ds)

    def load_tab(i):
        sch.op('sp', lambda s: [s.dma_start(out=sp.C, in_=rotC_d[:, i * T:(i + 1) * T]),
                                s.dma_start(out=sp.Sg, in_=rotS_d[:, i * T:(i + 1) * T])],
               reads=[r_rotC, r_rotS], writes=[sp.r_tab], dsem=sp.ds_tab, ndma=2)

    r_kms = Res('kms')
    r_scr = {n: Res('scr_' + n) for n in ('h', 'qT', 'kT', 'v', 'kT2', 'v2', 'at', 'km')}

    def qk_pieces(base):
        return [(base + 512, 512, 2), (base + 1024, 256, 6)]

    def phase_S1():
        new_sp_res()
        for i in range(NT):
            hb = i % 2
            hT, r_h = sp.hT[hb], sp.r_hT[hb]
            src = xT.rearrange("(c p) t -> p c t", p=128)[:, :, i * T:(i + 1) * T]
            sch.op('sp', lambda s, hT=hT, src=src: [s.dma_start(out=hT[:, 0:4, :], in_=src[:, 0:4, :]),
                                                     s.dma_start(out=hT[:, 4:8, :], in_=src[:, 4:8, :])],
                   writes=[r_h], dsem=sp.ds_hl[hb], ndma=2)
            load_tab(i)
            norm(hT, r_h, 0)
            ffn_steps(0, 1, hT, r_h)
            run_steps()
            sch.op('sp', lambda s, hT=hT, i=i: [s.dma_start(out=hscr[i], in_=hT)], reads=[r_h],
                   writes=[r_scr['h']], dsem=sp.ds_hs[hb])
            norm(hT, r_h, 1)
            qst, kst, vst = v3(sp.stage[0], 8), v3(sp.stage[1], 8), v3(sp.stage[2], 4)
            rot_steps(sp.hnT, sp.r_hn, 'wqkv', 0, qst, sp.r_stage[0])
            proj_fm_steps(sp.hnT, sp.r_hn, 'wqkv', qk_pieces(0), evac_copy(qst, sp.r_stage[0]))
            rot_steps(sp.hnT, sp.r_hn, 'wqkv', 1280, kst, sp.r_stage[1])
            proj_fm_steps(sp.hnT, sp.r_hn, 'wqkv', qk_pieces(1280), evac_copy(kst, sp.r_stage[1]))
            v_steps(sp.hnT, sp.r_hn, 'wqkv', 2560, vst, sp.r_stage[2])
            run_steps()
            store_fm(qst, sp.r_stage[0], sp.ds_st[0], qT_s, i)
            store_fm(kst, sp.r_stage[1], sp.ds_st[1], kT_s, i)
            store_v(vst, sp.r_stage[2], sp.ds_st[2], v_s, i)

    def att_head_loads(QA, KA, r_qk, ds, qsrc, ksrc, h):
        def fn(s):
            return [s.dma_start(out=QA[0:16, :], in_=qsrc[h * 16:(h + 1) * 16, :]),
                    s.dma_start(out=QA[16:64, :], in_=qsrc[256 + h * 48:256 + (h + 1) * 48, :]),
                    s.dma_start(out=KA[0:16, :], in_=ksrc[h * 16:(h + 1) * 16, :]),
                    s.dma_start(out=KA[16:64, :], in_=ksrc[256 + h * 48:256 + (h + 1) * 48, :])]
        sch.op('sp', fn, writes=[r_qk], dsem=ds, ndma=4)

    def phase_A():
        o = 0
        QA = [bview(o + k * 4096, 4096) for k in range(2)]; o += 8192
        KA = [bview(o + k * 4096, 4096) for k in range(2)]; o += 8192
        V1 = [[v3(bview(o + (k * 3 + b) * 2560, 2560), 32) for b in range(3)] for k in range(2)]; o += 6 * 2560
        PT = [bview(o + k * 256, 256) for k in range(4)]; o += 1024
        OST = [bview(o + k * 4096, 4096) for k in range(2)]; o += 8192
        ACC = [fview(k * 4096, 4096) for k in range(2)]
        r_qk = [Res('qk%d' % k) for k in range(2)]
        r_v = [Res('v1_%d' % k) for k in range(2)]
        r_pt = [Res('pt%d' % k) for k in range(4)]
        r_ost = [Res('ost%d' % k) for k in range(2)]
        r_acc = [Res('acc%d' % k) for k in range(2)]
        ds_qk = [sch.dsem('aqk') for k in range(2)]
        ds_v = [sch.dsem('av') for k in range(2)]
        ds_o = [sch.dsem('ao') for k in range(2)]
        r_ones = Res('v1ones')

        def f_ones(g):
            ins = None
            for k in range(2):
                for b in range(3):
                    ins = g.memset(V1[k][b][:, :, 64:65], 1.0)
            return ins
        sch.op('pool', f_ones, writes=r_v)
        DIL = (1, 4, 16)
        st = {'pt': 0, 'ps': 0, 'po': 0}

        def loads(h):
            k = h % 2
            att_head_loads(QA[k], KA[k], r_qk[k], ds_qk[k], qT_s, kT_s, h)

            def fn(s):
                ins = []
                for bi, d in enumerate(DIL):
                    nb = 32 // d
                    for r in range(d):
                        srcv = v_s[r:S:d, :].rearrange("(n p) c -> p n c", p=128)[:, :, h * 64:(h + 1) * 64]
                        if d == 1:
                            for q4 in range(4):
                                ins.append(s.dma_start(out=V1[k][bi][:, q4 * 8:(q4 + 1) * 8, 0:64],
                                                       in_=srcv[:, q4 * 8:(q4 + 1) * 8, :]))
                        else:
                            ins.append(s.dma_start(out=V1[k][bi][:, r * nb:(r + 1) * nb, 0:64], in_=srcv))
                return ins
            sch.op('sp', fn, reads=[r_scr['v']], writes=[r_v[k]], dsem=ds_v[k], ndma=4 + 4 + 16)

        loads(0)
        for h in range(H):
            k = h % 2
            if h + 1 < H:
                loads(h + 1)
            q, kk, acc = QA[k], KA[k], ACC[k]
            tiles = []
            for bi, d in enumerate(DIL):
                nb = 32 // d
                for r in range(d):
                    for n in range(nb):
                        tiles.append((bi, d, r, n, nb))
            info = {}

            def do_S(ti):
                bi, d, r, n, nb = tiles[ti]
                w = 256 if n + 1 < nb else 128
                b = st['ps'] % 3
                st['ps'] += 1
                base = r + d * 128 * n
                kap = kk[0:64, base:base + d * 127 + 1:d]
                qap = q[0:64, base:base + d * (w - 1) + 1:d]

                def f(t, b=b, w=w, kap=kap, qap=qap):
                    t.matmul(PS[b][:, 0:w], kap, qap, start=True, stop=False)
                    return t.matmul(PS[b][:, 0:w], ident[:], maskA[:, 0:w], start=False, stop=True)
                sch.op('pe', f, reads=[r_qk[k]] + CONST, writes=[PSR[b]])
                info[ti] = (b, w)

            def do_E(ti):
                b, w = info[ti]
                p = st['pt'] % 4
                st['pt'] += 1
                sch.op('act', lambda a_, b=b, w=w, p=p: a_.activation(out=PT[p][:, 0:w], in_=PS[b][:, 0:w], func=AF.Exp,
                                                                    scale=0.125),
                       reads=[PSR[b]], writes=[r_pt[p]])
                info[ti] = (b, w, p)

            def do_PV(ti):
                bi, d, r, n, nb = tiles[ti]
                b, w, p = info[ti]
                if n == 0:
                    st['po'] += 1
                g = n // 4
                ob = 3 + (st['po'] + g) % 2
                ob2 = 3 + (st['po'] + g + 1) % 2
                colA = (n % 4) * 128
                vap = V1[k][bi][:, r * nb + n, 0:65]
                wr = [PSR[ob]]
                nxt_new_bank = (n % 4 == 3)

                def f(t, ob=ob, ob2=ob2, colA=colA, vap=vap, p=p, w=w, n=n, nxt=nxt_new_bank):
                    ins = t.matmul(PS[ob][0:65, colA:colA + 128], vap, PT[p][:, 0:128], start=(n == 0), stop=True,
                                   skip_group_check=True)
                    if w == 256:
                        if nxt:
                            ins = t.matmul(PS[ob2][0:65, 0:128], vap, PT[p][:, 128:256], start=True, stop=False,
                                           skip_group_check=True)
                        else:
                            ins = t.matmul(PS[ob][0:65, colA + 128:colA + 256], vap, PT[p][:, 128:256], start=False,
                                           stop=False, skip_group_check=True)
                    return ins
                if w == 256 and nxt_new_bank:
                    wr.append(PSR[ob2])
                sch.op('pe', f, reads=[r_pt[p], r_v[k]], writes=wr)
                if n % 4 == 3 or n == nb - 1:
                    g0 = g * 4
                    wg = (n - g0 + 1) * 128
                    base = r + d * 128 * g0
                    accv = acc[0:65, base:base + d * (wg - 1) + 1:d]
                    if bi == 0:
                        sch.op('dve', lambda v, ob=ob, wg=wg, accv=accv: v.tensor_copy(out=accv, in_=PS[ob][0:65, 0:wg]),
                               reads=[PSR[ob]], writes=[r_acc[k]])
                    else:
                        sch.op('dve', lambda v, ob=ob, wg=wg, accv=accv: v.tensor_tensor(out=accv, in0=PS[ob][0:65, 0:wg],
                                                                                       in1=accv, op=ALU.add),
                               reads=[PSR[ob], r_acc[k]], writes=[r_acc[k]])
                if n == nb - 1:
                    st['po'] += (nb + 3) // 4 - 1

            LA = 2
            nt_ = len(tiles)
            for ti in range(min(LA, nt_)):
                do_S(ti)
            for ti in range(nt_):
                do_E(ti)
                if ti + LA < nt_:
                    do_S(ti + LA)
                do_PV(ti)
            sch.op('dve', lambda v, acc=acc: v.reciprocal(out=acc[64:65, :], in_=acc[64:65, :]),
                   reads=[r_acc[k]], writes=[r_acc[k]])
            ost = OST[k]
            for c8 in range(8):
                sl = slice(c8 * 512, (c8 + 1) * 512)
                sch.op('pe', lambda t, acc=acc, sl=sl: t.matmul(PS[7][0:64, :], ones_f[64:65, 0:64], acc[64:65, sl],
                                                              start=True, stop=True),
                       reads=[r_acc[k]] + CONST, writes=[PSR[7]])
                sch.op('dve', lambda v, acc=acc, sl=sl, ost=ost: v.tensor_tensor(out=ost[0:64, sl], in0=acc[0:64, sl],
                                                                                 in1=PS[7][0:64, :], op=ALU.mult),
                       reads=[r_acc[k], PSR[7]], writes=[r_ost[k]])
            sch.op('sp', lambda s, ost=ost, h=h: [s.dma_start(out=at_s[h * 64:(h + 1) * 64, :], in_=ost[0:64, :])],
                   reads=[r_ost[k]], writes=[r_scr['at']], dsem=ds_o[k])

    def oproj_steps(hT, r_h, wname, ast, r_ast):
        def ev(ch, b):
            sch.op('dve', lambda v, ch=ch, b=b: v.tensor_tensor(out=hT[:, ch, :], in0=PS[b][:], in1=hT[:, ch, :],
                                                               op=ALU.add),
                   reads=[PSR[b], r_h], writes=[r_h])
        proj_fm_steps(ast, r_ast, wname, [(0, 512, 0), (512, 512, 4)], ev)

    def load_h_at(i, hT, r_h, hb, ast, r_ast):
        sch.op('sp', lambda s: [s.dma_start(out=hT, in_=hscr[i])], reads=[r_scr['h']], writes=[r_h],
               dsem=sp.ds_hl[hb])
        src = at_s.rearrange("(c p) t -> p c t", p=128)[:, :, i * T:(i + 1) * T]
        sch.op('sp', lambda s: [s.dma_start(out=ast, in_=src)], reads=[r_scr['at']], writes=[r_ast], dsem=sp.ds_st[3])

    def phase_S2():
        new_sp_res()
        for i in range(NT):
            hb = i % 2
            hT, r_h = sp.hT[hb], sp.r_hT[hb]
            ast, r_ast = v3(sp.stage[3], 8), sp.r_stage[3]
            load_h_at(i, hT, r_h, hb, ast, r_ast)
            load_tab(i)
            oproj_steps(hT, r_h, 'woa', ast, r_ast)
            run_steps()
            norm(hT, r_h, 2)
            ffn_steps(0, 2, hT, r_h)
            run_steps()
            norm(hT, r_h, 3)
            qst, kst, vst = v3(sp.stage[0], 8), v3(sp.stage[1], 8), v3(sp.stage[2], 4)
            rot_steps(sp.hnT, sp.r_hn, 'wkv', 0, kst, sp.r_stage[1], kmean_tile=i)
            proj_fm_steps(sp.hnT, sp.r_hn, 'wkv', qk_pieces(0), evac_copy(kst, sp.r_stage[1], kmean_tile=i))
            v_steps(sp.hnT, sp.r_hn, 'wkv', 1280, vst, sp.r_stage[2])
            run_steps()
            store_fm(kst, sp.r_stage[1], sp.ds_st[1], kT2_s, i)
            store_v(vst, sp.r_stage[2], sp.ds_st[2], v2_s, i)
            norm(hT, r_h, 4)
            ffn_steps(1, 1, hT, r_h)
            run_steps()
            sch.op('sp', lambda s, hT=hT, i=i: [s.dma_start(out=hscr[i], in_=hT)], reads=[r_h],
                   writes=[r_scr['h']], dsem=sp.ds_hs[hb])
            norm(hT, r_h, 5)
            rot_steps(sp.hnT, sp.r_hn, 'wqb', 0, qst, sp.r_stage[0])
            proj_fm_steps(sp.hnT, sp.r_hn, 'wqb', qk_pieces(0), evac_copy(qst, sp.r_stage[0]))
            run_steps()
            store_fm(qst, sp.r_stage[0], sp.ds_st[0], qT_s, i)
        kmb = v3(sp.stage[0][:, 0:128], 8)
        sch.op('dve', lambda v: v.tensor_scalar(out=kmb, in0=kms[:], scalar1=1.0 / 256.0, scalar2=None, op0=ALU.mult),
               reads=[r_kms], writes=[sp.r_stage[0]])
        sch.op('sp', lambda s: [s.dma_start(out=km_s.rearrange("(c p) n -> p c n", p=128), in_=kmb)],
               reads=[sp.r_stage[0]], writes=[r_scr['km']], dsem=sp.ds_st[0])

    def phase_B():
        o = 0
        QA = [bview(o + k * 4096, 4096) for k in range(2)]; o += 8192
        KA = [bview(o + k * 4096, 4096) for k in range(2)]; o += 8192
        V1 = [v3(bview(o + k * 2560, 2560), 32) for k in range(2)]; o += 5120
        PT = [bview(o + k * 512, 512) for k in range(4)]; o += 2048
        OST = [bview(o + k * 4096, 4096) for k in range(2)]; o += 8192
        KM = [bview(o + k * 16, 16) for k in range(2)]; o += 32
        BB = v3(bview(o, 64), 4); o += 64
        f = 0
        ON = [fview(f + k * 512, 512) for k in range(2)]; f += 1024
        GM = v3(fview(f, 64), 4); f += 64
        MX = v3(fview(f, 32), 4); f += 32
        SEL = v3(fview(f, 64), 4); f += 64
        r_qk = [Res('bqk%d' % k) for k in range(2)]
        r_qb = [Res('bqb%d' % k) for k in range(2)]
        r_v = [Res('bv%d' % k) for k in range(2)]
        r_km = [Res('bkm%d' % k) for k in range(2)]
        r_pt = [Res('bpt%d' % k) for k in range(4)]
        r_ost = [Res('bost%d' % k) for k in range(2)]
        r_on = [Res('bon%d' % k) for k in range(2)]
        r_gm, r_mx, r_sel, r_bb = Res('gm'), Res('mx'), Res('sel'), Res('bb')
        ds_qk = [sch.dsem('bqk') for k in range(2)]
        ds_v = [sch.dsem('bv') for k in range(2)]
        ds_o = [sch.dsem('bo') for k in range(2)]
        ds_c = sch.dsem('bc')

        def f_ones(g):
            ins = None
            for k in range(2):
                ins = g.memset(V1[k][:, :, 64:65], 1.0)
            return ins
        sch.op('pool', f_ones, writes=r_v)
        sch.op('pool', lambda g: [g.dma_start(out=KA[0][64:80, :], in_=blk1h_d[:, :]),
                                  g.dma_start(out=KA[1][64:80, :], in_=blk1h_d[:, :])],
               writes=r_qk, dsem=ds_c, ndma=2)
        st = {'pt': 0, 'ps': 0, 'on': 0}

        def loads(h):
            k = h % 2
            att_head_loads(QA[k], KA[k], r_qk[k], ds_qk[k], qT_s, kT2_s, h)

            def fn(s):
                ins = []
                srcv = v2_s.rearrange("(n p) c -> p n c", p=128)[:, :, h * 64:(h + 1) * 64]
                for q4 in range(4):
                    ins.append(s.dma_start(out=V1[k][:, q4 * 8:(q4 + 1) * 8, 0:64], in_=srcv[:, q4 * 8:(q4 + 1) * 8, :]))
                ins.append(s.dma_start(out=KM[k][0:16, :], in_=km_s[h * 16:(h + 1) * 16, :]))
                ins.append(s.dma_start(out=KM[k][16:64, :], in_=km_s[256 + h * 48:256 + (h + 1) * 48, :]))
                return ins
            sch.op('sp', fn, reads=[r_scr['v2'], r_scr['km']], writes=[r_v[k], r_km[k]], dsem=ds_v[k], ndma=6)

        def gate(h, gq):
            k = h % 2
            q = QA[k]

            def f_g(t):
                ins = None
                for t4 in range(4):
                    c0 = gq * 512 + t4 * 128
                    ins = t.matmul(PS[6][:, t4 * 16:(t4 + 1) * 16], q[0:64, c0:c0 + 128], KM[k][0:64, :], start=True,
                                   stop=True, skip_group_check=True)
                return ins
            sch.op('pe', f_g, reads=[r_qk[k], r_km[k]], writes=[PSR[6]])
            sch.op('dve', lambda v: v.tensor_tensor(out=GM, in0=v3(PS[6][:, 0:64], 4), in1=v3(gbias[:, gq * 64:(gq + 1) * 64], 4),
                                                    op=ALU.add),
                   reads=[PSR[6]] + CONST, writes=[r_gm])

            def f_mx(v):
                ins = None
                for t4 in range(4):
                    ins = v.max(out=MX[:, t4, :], in_=GM[:, t4, :])
                return ins
            sch.op('dve', f_mx, reads=[r_gm], writes=[r_mx])
            sch.op('dve', lambda v: v.tensor_tensor(out=SEL, in0=GM, in1=MX[:, :, 3:4].to_broadcast([128, 4, 16]), op=ALU.is_ge),
                   reads=[r_gm, r_mx], writes=[r_sel])
            sch.op('dve', lambda v: v.tensor_scalar(out=BB, in0=SEL, scalar1=1.0, scalar2=-NEG, op0=ALU.subtract, op1=ALU.mult),
                   reads=[r_sel], writes=[r_bb])

            def f_t(t):
                ins = None
                for t4 in range(4):
                    ins = t.matmul(PS[6][0:16, 64 + t4 * 128:64 + (t4 + 1) * 128] if False else PS[7][0:16, t4 * 128:(t4 + 1) * 128],
                                   BB[:, t4, :], ident[:], start=True, stop=True, skip_group_check=True)
                return ins
            sch.op('pe', f_t, reads=[r_bb] + CONST, writes=[PSR[7]])
            sch.op('dve', lambda v: v.tensor_copy(out=q[64:80, gq * 512:(gq + 1) * 512], in_=PS[7][0:16, :]),
                   reads=[PSR[7]], writes=[r_qb[k]])

        loads(0)
        for gq in range(8):
            gate(0, gq)
        for h in range(H):
            k = h % 2
            if h + 1 < H:
                loads(h + 1)
            q, kk = QA[k], KA[k]
            tiles = []
            for gq in range(8):
                for kt in range(4 * gq + 4):
                    tiles.append((gq, kt))
            info = {}

            def do_S(ti):
                gq, kt = tiles[ti]
                b = st['ps'] % 3
                st['ps'] += 1
                diag = kt >= 4 * gq
                second = kt >= 4 * gq + 2
                kap = kk[0:80, kt * 128:(kt + 1) * 128]
                if second:
                    w = 256
                    qap = q[0:80, gq * 512 + 256:gq * 512 + 512]
                else:
                    w = 512
                    qap = q[0:80, gq * 512:gq * 512 + 512]
                mk = maskB[:, (kt % 2) * 256:(kt % 2) * 256 + 256]

                def f(t, b=b, w=w, kap=kap, qap=qap, diag=diag, mk=mk):
                    ins = t.matmul(PS[b][:, 0:w], kap, qap, start=True, stop=not diag)
                    if diag:
                        ins = t.matmul(PS[b][:, 0:256], ident[:], mk, start=False, stop=True)
                    return ins
                sch.op('pe', f, reads=[r_qk[k], r_qb[k]] + CONST, writes=[PSR[b]])
                info[ti] = (b, w)

            def do_E(ti):
                b, w = info[ti]
                p = st['pt'] % 4
                st['pt'] += 1
                sch.op('act', lambda a_, b=b, w=w, p=p: a_.activation(out=PT[p][:, 0:w], in_=PS[b][:, 0:w], func=AF.Exp,
                                                                    scale=0.125),
                       reads=[PSR[b]], writes=[r_pt[p]])
                info[ti] = (b, w, p)

            def do_PV(ti):
                gq, kt = tiles[ti]
                b, w, p = info[ti]
                ob = 3 + gq % 2
                c0 = 512 - w
                last = (kt == 4 * gq + 3)
                vap = V1[k][:, kt, 0:65]
                sch.op('pe', lambda t, ob=ob, c0=c0, vap=vap, p=p, w=w, kt=kt, last=last:
                       t.matmul(PS[ob][0:65, c0:512], vap, PT[p][:, 0:w], start=(kt == 0), stop=last, skip_group_check=True),
                       reads=[r_pt[p], r_v[k]], writes=[PSR[ob]])
                if last:
                    oi = st['on'] % 2
                    st['on'] += 1
                    on = ON[oi]
                    sch.op('act', lambda a_, ob=ob, on=on: a_.copy(out=on[0:65, :], in_=PS[ob][0:65, :]),
                           reads=[PSR[ob]], writes=[r_on[oi]])
                    sch.op('dve', lambda v, on=on: v.reciprocal(out=on[64:65, :], in_=on[64:65, :]),
                           reads=[r_on[oi]], writes=[r_on[oi]])
                    sch.op('pe', lambda t, on=on: t.matmul(PS[7][0:64, :], ones_f[64:65, 0:64], on[64:65, :], start=True,
                                                           stop=True),
                           reads=[r_on[oi]] + CONST, writes=[PSR[7]])
                    ost = OST[k]
                    sch.op('dve', lambda v, on=on, ost=ost, gq=gq: v.tensor_tensor(out=ost[0:64, gq * 512:(gq + 1) * 512],
                                                                                   in0=on[0:64, :], in1=PS[7][0:64, :],
                                                                                   op=ALU.mult),
                           reads=[r_on[oi], PSR[7]], writes=[r_ost[k]])
                    if h + 1 < H:
                        gate(h + 1, gq)

            LA = 2
            nt_ = len(tiles)
            for ti in range(min(LA, nt_)):
                do_S(ti)
            for ti in range(nt_):
                do_E(ti)
                if ti + LA < nt_:
                    do_S(ti + LA)
                do_PV(ti)
            ost = OST[k]
            sch.op('sp', lambda s, ost=ost, h=h: [s.dma_start(out=at_s[h * 64:(h + 1) * 64, :], in_=ost[0:64, :])],
                   reads=[r_ost[k]], writes=[r_scr['at']], dsem=ds_o[k])

    def phase_S3():
        new_sp_res()
        ds_y = [sch.dsem('y') for k in range(2)]
        r_y = Res('y')
        for i in range(NT):
            hb = i % 2
            hT, r_h = sp.hT[hb], sp.r_hT[hb]
            ast, r_ast = v3(sp.stage[3], 8), sp.r_stage[3]
            load_h_at(i, hT, r_h, hb, ast, r_ast)
            oproj_steps(hT, r_h, 'wob', ast, r_ast)
            run_steps()
            norm(hT, r_h, 6)
            ffn_steps(1, 2, hT, r_h)
            run_steps()
            norm(hT, r_h, 7, final=True)
            dst = yT.rearrange("(c p) t -> p c t", p=128)[:, :, i * T:(i + 1) * T]
            sch.op('sp', lambda s, hT=hT, dst=dst: [s.dma_start(out=dst[:, 0:4, :], in_=hT[:, 0:4, :]),
                                                     s.dma_start(out=dst[:, 4:8, :], in_=hT[:, 4:8, :])],
                   reads=[r_h], writes=[r_y], dsem=ds_y[hb], ndma=2)

    phases = [('S1', phase_S1), ('A', phase_A), ('S2', phase_S2), ('B', phase_B), ('S3', phase_S3)]
    for name, fn in phases:
        fn()
        sch.barrier()
        if stop_after == name:
            break
    sch.emit()
    return nc


def _host_consts():
    half = 8
    inv_freq = (500000.0 ** (-np.arange(half, dtype=np.float32) / half)).astype(np.float32)
    pos = np.arange(S, dtype=np.float32)
    ang = pos[:, None] * inv_freq[None, :]
    cos, sin = np.cos(ang).astype(np.float32), np.sin(ang).astype(np.float32)
    rotC = np.zeros((128, S), np.float32)
    rotS = np.zeros((128, S), np.float32)
    for hh in range(8):
        for j in range(16):
            p = hh * 16 + j
            rotC[p] = cos[:, j % 8]
            rotS[p] = -sin[:, j] if j < 8 else sin[:, j - 8]
    j = np.arange(128)[:, None]
    i = np.arange(256)[None, :]
    allowedA = np.where(i < 128, j <= i, j >= (i - 128))
    maskA = np.where(allowedA, 0.0, NEG).astype(np.float32)
    cm0 = np.where(j <= i, 0.0, NEG)
    cm1 = np.where(i >= 128 + j, 0.0, NEG)
    maskB = np.concatenate([cm0, cm1], axis=1).astype(np.float32)
    ident = np.eye(128, dtype=np.float32)
    blk1h = np.zeros((16, S), np.float32)
    for n in range(16):
        blk1h[n, n * 256:(n + 1) * 256] = 1.0
    gb = np.zeros((8, 4, 16), np.float32)
    for gq in range(8):
        for t4 in range(4):
            own = 2 * gq + t4 // 2
            gb[gq, t4, own] = 1e30
            gb[gq, t4, own + 1:] = -1e30
    gbias = np.broadcast_to(gb.reshape(1, 512), (128, 512)).copy()
    return dict(rotC=rotC, rotS=rotS, maskA=maskA, maskB=maskB, ident=ident, blk1h=blk1h, gbias=gbias)


def _perm_cols():
    rot = np.array([h * 64 + j for h in range(H) for j in range(16)])
    swp = np.array([h * 64 + ((j + 8) % 16) for h in range(H) for j in range(16)])
    non = np.array([h * 64 + j for h in range(H) for j in range(16, 64)])
    return np.concatenate([rot, swp, non])


def _prep_inputs(inp):
    f = lambda a: np.ascontiguousarray(np.asarray(a), dtype=np.float32)
    pc = _perm_cols()
    shared = {}
    for l in (0, 1):
        shared['g%d1' % l] = f(inp['ffn1_w_gate'][l])
        shared['u%d1' % l] = f(inp['ffn1_w_up'][l])
        shared['d%d1' % l] = f(inp['ffn1_w_down'][l])
        shared['g%d2' % l] = f(inp['ffn2_w_gate'][l])
        shared['u%d2' % l] = f(inp['ffn2_w_up'][l])
        shared['d%d2' % l] = f(inp['ffn2_w_down'][l])
    wqkv = np.asarray(inp['a_w_qkv'][0])
    wq, wk, wv = wqkv[:, 0:1024], wqkv[:, 1024:2048], wqkv[:, 2048:3072]
    shared['wqkv'] = f(np.concatenate([wq[:, pc], wk[:, pc], wv], axis=1))
    shared['woa'] = f(inp['a_w_o'][0])
    kvw = np.asarray(inp['kv_w'])
    shared['wkv'] = f(np.concatenate([kvw[:, 0:1024][:, pc], kvw[:, 1024:2048]], axis=1))
    shared['wqb'] = f(np.asarray(inp['b_w_q'][0])[:, pc])
    shared['wob'] = f(inp['b_w_o'][0])
    gl = [inp['ffn1_norm'][0], inp['mix_norm'][0], inp['ffn2_norm'][0], inp['kv_norm'],
          inp['ffn1_norm'][1], inp['mix_norm'][1], inp['ffn2_norm'][1], inp['final_norm']]
    g = np.stack([np.asarray(a, dtype=np.float32) for a in gl], 0)
    shared['gains'] = f(g.reshape(8, 8, 128).transpose(2, 0, 1).reshape(128, 64))
    shared.update(_host_consts())
    return shared


def kernel(**inputs):
    x = np.asarray(inputs['x'], dtype=np.float32)
    shared = _prep_inputs(inputs)
    sliced = set(WORDER) | {'rotC', 'rotS'}
    in_maps = []
    for b in range(8):
        m = {}
        for k_, v_ in shared.items():
            if k_ in sliced:
                r8 = v_.shape[0] // 8
                m[k_] = np.ascontiguousarray(v_[b * r8:(b + 1) * r8])
            else:
                m[k_] = v_
        m['xT'] = np.ascontiguousarray(x[b].T)
        in_maps.append(m)
    nc = build()
    res = run_bass_kernel_spmd(nc, in_maps, core_ids=list(range(8)))
    out = np.stack([np.ascontiguousarray(res.results[b]['yT'].T) for b in range(8)], 0)
    return out.astype(np.float32)
```
